# Optimizing a Trainium2 kernel written in Bass

```python
import math
import jax
import jax.numpy as jnp
from jax import lax
import numpy as np

D_MODEL = 2048
BATCH = 4
SEQ = 4096
DEPTH = 4

MEM_LEN = 256
N_EVEN = (DEPTH + 1) // 2
N_ODD = DEPTH // 2
CONV_K = 4
LRU_WIDTH = D_MODEL // 2
LRU_BLOCKS = 8
LRU_BLOCK = LRU_WIDTH // LRU_BLOCKS
LRU_C = 8.0
HGRN_WIDTH = D_MODEL // 2
HGRN_HEADS = 8
HGRN_DK = HGRN_WIDTH // HGRN_HEADS
HGRN_DV = HGRN_DK
HGRN_CHUNK = 64
AB_IN = 2 * LRU_WIDTH + 4 * HGRN_WIDTH
AB_SPLITS = [LRU_WIDTH, 2 * LRU_WIDTH, 2 * LRU_WIDTH + HGRN_WIDTH,
             2 * LRU_WIDTH + 2 * HGRN_WIDTH, 2 * LRU_WIDTH + 3 * HGRN_WIDTH]
AB_OUT = LRU_WIDTH + HGRN_WIDTH
SSD_INNER = 2 * D_MODEL
SSD_HEADDIM = 64
SSD_HEADS = SSD_INNER // SSD_HEADDIM
SSD_GROUPS = 8
SSD_HPG = SSD_HEADS // SSD_GROUPS
SSD_STATE = 128
SSD_CHUNK = 128
SSD_CONV_DIM = SSD_INNER + 2 * SSD_GROUPS * SSD_STATE
SSD_IN = SSD_INNER + SSD_CONV_DIM + SSD_HEADS
XA_HEADS = 4
XA_HEADDIM = D_MODEL // XA_HEADS
FFN_HIDDEN = ((8 * D_MODEL // 3 + 255) // 256) * 256

kernel_name = 'hybrid_rglru_hgrn2_ssd_trunk'


def rmsnorm(x, g, eps=1e-6):
    xf = x.astype(jnp.float32)
    y = xf * lax.rsqrt(jnp.mean(xf * xf, axis=-1, keepdims=True) + eps)
    return (y * g.astype(jnp.float32)).astype(x.dtype)


def causal_dwconv(u, w, b):
    k_width = w.shape[0]
    s = u.shape[1]
    up = jnp.pad(u, ((0, 0), (k_width - 1, 0), (0, 0)))
    out = b
    for k in range(k_width):
        out = out + up[:, k:k + s] * w[k]
    return out


def linear_recurrence(a, b):
    def combine(l, r):
        return (l[0] * r[0], r[0] * l[1] + r[1])
    _, h = lax.associative_scan(combine, (a, b), axis=1)
    return h


def hgrn2_chunked(q, k, log_f, v):
    bsz, s, h, dk = q.shape
    dv = v.shape[-1]
    n_chunks = s // HGRN_CHUNK

    def to_chunks(t):
        return t.reshape(bsz, n_chunks, HGRN_CHUNK, h, t.shape[-1]).transpose(1, 0, 3, 2, 4)

    causal = jnp.tril(jnp.ones((HGRN_CHUNK, HGRN_CHUNK), dtype=bool))[:, :, None]

    def step(state, inp):
        qc, kc, gc, vc = inp
        cum = jnp.cumsum(gc, axis=2)
        diff = cum[:, :, :, None, :] - cum[:, :, None, :, :]
        decay = jnp.exp(jnp.where(causal, diff, -jnp.inf))
        scores = jnp.einsum('bhtk,bhsk,bhtsk->bhts', qc, kc, decay)
        o = jnp.einsum('bhts,bhsv->bhtv', scores, vc)
        o = o + jnp.einsum('bhtk,bhkv->bhtv', qc * jnp.exp(cum), state)
        last = cum[:, :, -1:, :]
        state = (jnp.exp(last[:, :, 0, :, None]) * state
                 + jnp.einsum('bhsk,bhsv->bhkv', kc * jnp.exp(last - cum), vc))
        return state, o

    state0 = jnp.zeros((bsz, h, dk, dv), jnp.float32)
    xs = (to_chunks(q), to_chunks(k), to_chunks(log_f), to_chunks(v))
    _, o = lax.scan(step, state0, xs)
    return o.transpose(1, 0, 3, 2, 4).reshape(bsz, s, h, dv)


def ssd_chunked(xh, dt, a_neg, bm, cm):
    bsz, s, g, r, p = xh.shape
    n = bm.shape[-1]
    n_chunks = s // SSD_CHUNK

    def to_chunks(t):
        return jnp.moveaxis(t.reshape(bsz, n_chunks, SSD_CHUNK, *t.shape[2:]), 1, 0)

    causal = jnp.tril(jnp.ones((SSD_CHUNK, SSD_CHUNK), dtype=bool))

    def step(state, inp):
        xc, dtc, bc, cc = inp
        cum = jnp.cumsum(dtc * a_neg, axis=1)
        cum_h = jnp.moveaxis(cum, 1, -1)
        seg = cum_h[..., :, None] - cum_h[..., None, :]
        decay = jnp.exp(jnp.where(causal, seg, -jnp.inf))
        cb = jnp.einsum('btgn,bsgn->bgts', cc, bc)
        xdt = xc * dtc[..., None]
        y = jnp.einsum('bgrts,bsgrp->btgrp', cb[:, :, None] * decay, xdt)
        y = y + jnp.einsum('btgn,bgrpn->btgrp', cc, state) * jnp.exp(cum)[..., None]
        last = cum[:, -1]
        w_s = jnp.exp(last[:, None] - cum)[..., None]
        state = (jnp.exp(last)[..., None, None] * state
                 + jnp.einsum('bsgn,bsgrp->bgrpn', bc, xdt * w_s))
        return state, y

    state0 = jnp.zeros((bsz, g, r, p, n), jnp.float32)
    xs = (to_chunks(xh), to_chunks(dt), to_chunks(bm), to_chunks(cm))
    _, y = lax.scan(step, state0, xs)
    return jnp.moveaxis(y, 0, 1).reshape(bsz, s, g, r, p)


def rglru_hgrn2_mixer(h, w_in, w_out, conv_w, conv_b, w_r, b_r, w_i, b_i, lam,
                      lower_bound, head_norm):
    bsz, s, _ = h.shape
    f32 = jnp.float32
    proj = h @ w_in
    xa, ga, q, f, iv, gb = jnp.split(proj, AB_SPLITS, axis=-1)
    xa = causal_dwconv(xa, conv_w, conv_b)
    xb = xa.reshape(bsz, s, LRU_BLOCKS, LRU_BLOCK)
    r_gate = jax.nn.sigmoid(jnp.einsum('bshi,hij->bshj', xb, w_r).reshape(bsz, s, LRU_WIDTH) + b_r).astype(f32)
    i_gate = jax.nn.sigmoid(jnp.einsum('bshi,hij->bshj', xb, w_i).reshape(bsz, s, LRU_WIDTH) + b_i).astype(f32)
    log_a = -LRU_C * r_gate * jax.nn.softplus(-lam.astype(f32))
    a = jnp.exp(log_a)
    mult = jnp.sqrt(-jnp.expm1(2.0 * log_a))
    h_lru = linear_recurrence(a, mult * i_gate * xa.astype(f32))
    y_a = jax.nn.gelu(ga) * h_lru.astype(h.dtype)
    qh = jax.nn.silu(q).astype(f32).reshape(bsz, s, HGRN_HEADS, HGRN_DK)
    fg = lower_bound + (1.0 - lower_bound) * jax.nn.sigmoid(f.astype(f32))
    kh = (1.0 - fg).reshape(bsz, s, HGRN_HEADS, HGRN_DK)
    log_fg = jnp.log(fg).reshape(bsz, s, HGRN_HEADS, HGRN_DK)
    vh = iv.astype(f32).reshape(bsz, s, HGRN_HEADS, HGRN_DV)
    o = hgrn2_chunked(qh, kh, log_fg, vh)
    o = rmsnorm(o, head_norm.reshape(HGRN_HEADS, HGRN_DV)).reshape(bsz, s, HGRN_WIDTH)
    y_b = o.astype(h.dtype) * jax.nn.silu(gb)
    return jnp.concatenate([y_a, y_b], axis=-1) @ w_out


def ssd_mixer(h, w_in, w_out, conv_w, conv_b, dt_bias, a_log, d_skip, norm_w):
    bsz, s, _ = h.shape
    f32 = jnp.float32
    proj = h @ w_in
    z, xbc, dt_raw = jnp.split(proj, [SSD_INNER, SSD_INNER + SSD_CONV_DIM], axis=-1)
    xbc = jax.nn.silu(causal_dwconv(xbc, conv_w, conv_b))
    xs, bm, cm = jnp.split(xbc, [SSD_INNER, SSD_INNER + SSD_GROUPS * SSD_STATE], axis=-1)
    dt = jax.nn.softplus(dt_raw.astype(f32) + dt_bias.astype(f32))
    a_neg = -jnp.exp(a_log.astype(f32))
    xh = xs.astype(f32).reshape(bsz, s, SSD_GROUPS, SSD_HPG, SSD_HEADDIM)
    y = ssd_chunked(xh,
                    dt.reshape(bsz, s, SSD_GROUPS, SSD_HPG),
                    a_neg.reshape(SSD_GROUPS, SSD_HPG),
                    bm.astype(f32).reshape(bsz, s, SSD_GROUPS, SSD_STATE),
                    cm.astype(f32).reshape(bsz, s, SSD_GROUPS, SSD_STATE))
    y = y + d_skip.astype(f32).reshape(SSD_GROUPS, SSD_HPG)[:, :, None] * xh
    y = y.reshape(bsz, s, SSD_INNER).astype(h.dtype) * jax.nn.silu(z)
    y = rmsnorm(y.reshape(bsz, s, SSD_GROUPS, SSD_INNER // SSD_GROUPS),
                norm_w.reshape(SSD_GROUPS, SSD_INNER // SSD_GROUPS)).reshape(bsz, s, SSD_INNER)
    return y @ w_out


def memory_cross_attention(h, mem_n, w_q, w_kv, w_o):
    bsz, s, _ = h.shape
    q = (h @ w_q).reshape(bsz, s, XA_HEADS, XA_HEADDIM)
    k, v = jnp.split(mem_n @ w_kv, 2, axis=-1)
    k = k.reshape(bsz, -1, XA_HEADS, XA_HEADDIM)
    v = v.reshape(bsz, -1, XA_HEADS, XA_HEADDIM)
    scores = jnp.einsum('bshd,bmhd->bhsm', q, k).astype(jnp.float32) * (XA_HEADDIM ** -0.5)
    p = jax.nn.softmax(scores, axis=-1).astype(v.dtype)
    o = jnp.einsum('bhsm,bmhd->bshd', p, v).reshape(bsz, s, D_MODEL)
    return o @ w_o


def swiglu(h, w_gate, w_up, w_down):
    return (jax.nn.silu(h @ w_gate) * (h @ w_up)) @ w_down


def setup_inputs(seed: int = 0) -> dict:
    key = jax.random.key(seed)
    k = jax.random.split(key, 32)
    f32 = jnp.float32

    def nrm(i, shape, scale):
        return jax.random.normal(k[i], shape, f32) * scale

    def gain(i, shape):
        return 1.0 + nrm(i, shape, 0.02)

    lam_u = jax.random.uniform(k[16], (N_EVEN, LRU_WIDTH), f32, 0.9, 0.999)
    lam_s = lam_u ** (1.0 / LRU_C)
    lam = jnp.log(lam_s) - jnp.log1p(-lam_s)
    dt0 = jnp.exp(jax.random.uniform(k[22], (N_ODD, SSD_HEADS), f32, math.log(1e-3), math.log(1e-1)))
    dt_bias = dt0 + jnp.log(-jnp.expm1(-dt0))
    a_log = jnp.log(jax.random.uniform(k[23], (N_ODD, SSD_HEADS), f32, 1.0, 16.0))
    return {
        'x': nrm(0, (BATCH, SEQ, D_MODEL), 1.0),
        'mem': nrm(1, (BATCH, MEM_LEN, D_MODEL), 1.0),
        'norm_mix': gain(2, (DEPTH, D_MODEL)),
        'norm_xattn': gain(3, (DEPTH, D_MODEL)),
        'norm_ffn': gain(4, (DEPTH, D_MODEL)),
        'norm_mem': gain(5, (D_MODEL,)),
        'norm_final': gain(6, (D_MODEL,)),
        'ab_w_in': nrm(7, (N_EVEN, D_MODEL, AB_IN), D_MODEL ** -0.5),
        'ab_w_out': nrm(8, (N_EVEN, AB_OUT, D_MODEL), AB_OUT ** -0.5),
        'lru_conv_w': nrm(9, (N_EVEN, CONV_K, LRU_WIDTH), CONV_K ** -0.5),
        'lru_conv_b': nrm(10, (N_EVEN, LRU_WIDTH), 0.01),
        'lru_w_r': nrm(11, (N_EVEN, LRU_BLOCKS, LRU_BLOCK, LRU_BLOCK), LRU_BLOCK ** -0.5),
        'lru_b_r': nrm(12, (N_EVEN, LRU_WIDTH), 0.01),
        'lru_w_i': nrm(13, (N_EVEN, LRU_BLOCKS, LRU_BLOCK, LRU_BLOCK), LRU_BLOCK ** -0.5),
        'lru_b_i': nrm(14, (N_EVEN, LRU_WIDTH), 0.01),
        'lru_lambda': lam,
        'hgrn_lower_bounds': nrm(15, (N_EVEN, HGRN_WIDTH), 0.1),
        'hgrn_norm': gain(17, (N_EVEN, HGRN_WIDTH)),
        'ssd_w_in': nrm(18, (N_ODD, D_MODEL, SSD_IN), D_MODEL ** -0.5),
        'ssd_w_out': nrm(19, (N_ODD, SSD_INNER, D_MODEL), SSD_INNER ** -0.5),
        'ssd_conv_w': nrm(20, (N_ODD, CONV_K, SSD_CONV_DIM), CONV_K ** -0.5),
        'ssd_conv_b': nrm(21, (N_ODD, SSD_CONV_DIM), 0.01),
        'ssd_dt_bias': dt_bias,
        'ssd_a_log': a_log,
        'ssd_d': gain(24, (N_ODD, SSD_HEADS)),
        'ssd_norm': gain(25, (N_ODD, SSD_INNER)),
        'xa_w_q': nrm(26, (DEPTH, D_MODEL, D_MODEL), D_MODEL ** -0.5),
        'xa_w_kv': nrm(27, (DEPTH, D_MODEL, 2 * D_MODEL), D_MODEL ** -0.5),
        'xa_w_o': nrm(28, (DEPTH, D_MODEL, D_MODEL), D_MODEL ** -0.5),
        'ffn_w_gate': nrm(29, (DEPTH, D_MODEL, FFN_HIDDEN), D_MODEL ** -0.5),
        'ffn_w_up': nrm(30, (DEPTH, D_MODEL, FFN_HIDDEN), D_MODEL ** -0.5),
        'ffn_w_down': nrm(31, (DEPTH, FFN_HIDDEN, D_MODEL), FFN_HIDDEN ** -0.5),
    }


def reference(x, mem, norm_mix, norm_xattn, norm_ffn, norm_mem, norm_final,
              ab_w_in, ab_w_out, lru_conv_w, lru_conv_b, lru_w_r, lru_b_r, lru_w_i, lru_b_i,
              lru_lambda, hgrn_lower_bounds, hgrn_norm,
              ssd_w_in, ssd_w_out, ssd_conv_w, ssd_conv_b, ssd_dt_bias, ssd_a_log, ssd_d, ssd_norm,
              xa_w_q, xa_w_kv, xa_w_o, ffn_w_gate, ffn_w_up, ffn_w_down):
    sm = jax.nn.softmax(hgrn_lower_bounds.astype(jnp.float32), axis=0)
    lower_bounds = jnp.cumsum(sm, axis=0) - sm[0]
    mem_n = rmsnorm(mem, norm_mem)
    for layer in range(DEPTH):
        h = rmsnorm(x, norm_mix[layer])
        if layer % 2 == 0:
            e = layer // 2
            y = rglru_hgrn2_mixer(h, ab_w_in[e], ab_w_out[e], lru_conv_w[e], lru_conv_b[e],
                                  lru_w_r[e], lru_b_r[e], lru_w_i[e], lru_b_i[e], lru_lambda[e],
                                  lower_bounds[e], hgrn_norm[e])
        else:
            o = layer // 2
            y = ssd_mixer(h, ssd_w_in[o], ssd_w_out[o], ssd_conv_w[o], ssd_conv_b[o],
                          ssd_dt_bias[o], ssd_a_log[o], ssd_d[o], ssd_norm[o])
        x = x + y
        x = x + memory_cross_attention(rmsnorm(x, norm_xattn[layer]), mem_n,
                                       xa_w_q[layer], xa_w_kv[layer], xa_w_o[layer])
        x = x + swiglu(rmsnorm(x, norm_ffn[layer]), ffn_w_gate[layer], ffn_w_up[layer], ffn_w_down[layer])
    return rmsnorm(x, norm_final)
```

```python
import numpy as np
from contextlib import ExitStack
import concourse.bass as bass
import concourse.mybir as mybir
from concourse.bass_utils import run_bass_kernel_spmd

F32 = mybir.dt.float32
BF16 = mybir.dt.bfloat16
AF = mybir.ActivationFunctionType
ALU = mybir.AluOpType
AX = mybir.AxisListType

P = 128
D = 2048
KC = 16
MEM = 256
FFN = 5632
FC = FFN // P
SSD_IN = 10304
ENG = ("pe", "act", "dve", "pool", "sp")


class Buf:
    __slots__ = ("t", "w", "r", "name", "excl")

    def __init__(self, t, name="", excl=False):
        self.t = t
        self.w = None
        self.r = {}
        self.name = name
        self.excl = excl


class Sched:
    def __init__(self, nc, es, nds_sp=28, nds_pool=3):
        self.nc = nc
        self.prog = {e: [] for e in ENG}
        self.sem = {e: es.enter_context(nc.semaphore("s_" + e)) for e in ENG}
        self.cnt = {e: 0 for e in ENG}
        self.seen = {e: {} for e in ENG}
        self.dsem = []
        self.dval = []
        self.dpool = {"sp": [], "pool": [], "act": []}
        for q, n in (("sp", nds_sp), ("pool", nds_pool)):
            for i in range(n):
                self.dpool[q].append(len(self.dsem))
                self.dsem.append(es.enter_context(nc.semaphore("d_%s%d" % (q, i))))
                self.dval.append(0)
        self.dnext = {"sp": 0, "pool": 0}
        self.ninst = 0

    def _wait(self, e, key, v):
        if self.seen[e].get(key, 0) >= v:
            return
        self.seen[e][key] = v
        sem = self.sem[key] if isinstance(key, str) else self.dsem[key[1]]
        self.prog[e].append(lambda h, sem=sem, v=v: h.wait_ge(sem, v))
        self.ninst += 1

    def _deps(self, e, reads, writes):
        need = {}
        for b in reads:
            if b.w is not None:
                k, v = b.w
                if need.get(k, 0) < v:
                    need[k] = v
            if getattr(b, "excl", False):
                for (k, v) in b.r.values():
                    if k != e and need.get(k, 0) < v:
                        need[k] = v
        for b in writes:
            if b.w is not None:
                k, v = b.w
                if need.get(k, 0) < v:
                    need[k] = v
            for (k, v) in b.r.values():
                if need.get(k, 0) < v:
                    need[k] = v
        for k, v in need.items():
            self._wait(e, k, v)

    def _mark(self, tok, rkey, reads, writes):
        for b in reads:
            b.r[rkey] = tok
        for b in writes:
            b.w = tok
            b.r = {}

    def op(self, e, fn, reads=(), writes=()):
        self._deps(e, reads, writes)
        self.cnt[e] += 1
        tok = (e, self.cnt[e])
        sem = self.sem[e]
        self.prog[e].append(lambda h, fn=fn, sem=sem: fn(h).then_inc(sem, 1))
        self.ninst += 1
        self._mark(tok, e, reads, writes)

    def dma(self, q, out, in_, reads=(), writes=()):
        pool = self.dpool[q]
        i = pool[self.dnext[q]]
        self.dnext[q] = (self.dnext[q] + 1) % len(pool)
        if self.dval[i]:
            self._wait(q, ("d", i), self.dval[i])
        self._deps(q, reads, writes)
        self.dval[i] += 16
        tok = (("d", i), self.dval[i])
        sem = self.dsem[i]
        self.prog[q].append(lambda h, o=out, a=in_, sem=sem: h.dma_start(out=o, in_=a).then_inc(sem, 16))
        self.ninst += 1
        self._mark(tok, ("d", i), reads, writes)

    def barrier(self):
        for e in ENG:
            for e2 in ENG:
                if e2 != e and self.cnt[e2]:
                    self._wait(e, e2, self.cnt[e2])
            for i, v in enumerate(self.dval):
                if v:
                    self._wait(e, ("d", i), v)

    def finish(self):
        for i, v in enumerate(self.dval):
            if v:
                self._wait("sp", ("d", i), v)
        for e in ENG:
            if e != "sp" and self.cnt[e]:
                self._wait("sp", e, self.cnt[e])

    def emit(self, block):
        def runner(e):
            def f(h):
                for th in self.prog[e]:
                    th(h)
            return f
        block.tensor(runner("pe"))
        block.scalar(runner("act"))
        block.vector(runner("dve"))
        block.gpsimd(runner("pool"))
        block.sync(runner("sp"))


def bcast(ap, axis, n):
    a = ap.unsqueeze(axis)
    shp = list(a.shape)
    shp[axis] = n
    return a.to_broadcast(shp)


def make_consts(T):
    k = np.arange(P)
    ident = np.eye(P, dtype=np.float32)
    causal = (k[:, None] <= k[None, :]).astype(np.float32)
    lstrict = (k[:, None] > k[None, :]).astype(np.float32)
    esel = np.zeros((P, P), np.float32)
    esel[P - 1, :] = 1.0
    ones = np.ones((P, P), np.float32)
    rmask = np.ones((P, T), np.float32)
    rmask[:, ::P] = 0.0
    return np.concatenate([ident, causal, lstrict, esel, ones, rmask], axis=1)


C_ID, C_CAUS, C_LST, C_ESEL, C_ONES, C_RM = 0, 128, 256, 384, 512, 640


def build(S, layers, T=128, STAGES="mxf", DBG=""):
    NT = S // T
    NCH = T // P
    nc = bass.Bass("TRN2", target_bir_lowering=False)
    es = ExitStack()
    es.__enter__()
    sc = Sched(nc, es)

    def din(name, shape, dt=F32):
        return nc.dram_tensor(name, list(shape), dt, kind="ExternalInput").ap()

    xT_in = din("xT_in", [KC, P, S])
    memT_in = din("memT_in", [KC, P, MEM])
    consts_in = din("consts_in", [P, C_RM + T])
    g_mix = din("g_mix", [4, P, KC]); g_xat = din("g_xat", [4, P, KC]); g_ffn = din("g_ffn", [4, P, KC])
    g_mem = din("g_mem", [P, KC]); g_fin = din("g_fin", [P, KC])
    ab_w_in = din("ab_w_in", [2, D, 6144]); ab_w_out = din("ab_w_out", [2, D, D])
    lru_cw = din("lru_cw", [2, P, 8, 4]); lru_cb = din("lru_cb", [2, P, 8])
    lru_wr = din("lru_wr", [2, 8, P, P]); lru_br = din("lru_br", [2, P, 8])
    lru_wi = din("lru_wi", [2, 8, P, P]); lru_bi = din("lru_bi", [2, P, 8])
    lru_lam = din("lru_lam", [2, P, 8]); hg_lb = din("hg_lb", [2, P, 8]); hg_norm = din("hg_norm", [2, 1024])
    ssd_w_in = din("ssd_w_in", [2, D, SSD_IN]); ssd_w_out = din("ssd_w_out", [2, 4096, D])
    ssd_cw = din("ssd_cw", [2, P, 48, 4]); ssd_cb = din("ssd_cb", [2, P, 48])
    ssd_dtb = din("ssd_dtb", [2, P, 1]); ssd_alog = din("ssd_alog", [2, P, 1])
    ssd_d = din("ssd_d", [2, 64]); ssd_nwT = din("ssd_nwT", [2, P, 32])
    xa_wq = din("xa_wq", [4, D, D]); xa_wkv = din("xa_wkv", [4, D, 2 * D]); xa_wo = din("xa_wo", [4, D, D])
    ffn_wg = din("ffn_wg", [4, D, FFN]); ffn_wu = din("ffn_wu", [4, D, FFN]); ffn_wd = din("ffn_wd", [4, FFN, D])
    out_T = nc.dram_tensor("out_T", [KC, P, S], F32, kind="ExternalOutput").ap()

    def dscr(name, shape, dt):
        return Buf(nc.dram_tensor(name, list(shape), dt, kind="Internal").ap(), name)

    xT_d = dscr("xT_d", [KC, P, S], F32)

    def sb(name, shape, dt=F32):
        return Buf(es.enter_context(nc.sbuf_tensor(name, list(shape), dt)), name)

    wbf = {}

    def precast(key, src2d, K, N, ncols, n_used=None):
        kct = K // P
        n_used = N if n_used is None else n_used
        nblk = n_used // ncols
        dst = dscr("wbf_" + key, [nblk, P, kct, ncols], BF16)
        sv = src2d.rearrange("(c p) n -> p c n", p=P)
        for b_ in range(nblk):
            sc.dma("pool", dst.t[b_], sv[:, :, b_ * ncols:(b_ + 1) * ncols], writes=[dst])
        wbf[key] = (dst, ncols)

    def precast_plain(key, src2d, K, N):
        dst = dscr("wbf_" + key, [K, N], BF16)
        sc.dma("pool", dst.t, src2d, writes=[dst])
        wbf[key] = (dst, None)

    for l in layers:
        e = l // 2
        if l % 2 == 0:
            precast("min%d" % l, ab_w_in[e], D, 6144, 512)
            precast("mout%d" % l, ab_w_out[e], D, D, 512)
        else:
            precast("min%d" % l, ssd_w_in[e], D, SSD_IN, 512, n_used=10240)
            precast_plain("wdt%d" % l, ssd_w_in[e][:, 10240:10304], D, 64)
            precast("mout%d" % l, ssd_w_out[e], 4096, D, 512)
        precast("wq%d" % l, xa_wq[l], D, D, 512)
        precast("wkv%d" % l, xa_wkv[l], D, 2 * D, 512)
        precast("wo%d" % l, xa_wo[l], D, D, 512)
        precast("wg%d" % l, ffn_wg[l], D, FFN, 512)
        precast("wu%d" % l, ffn_wu[l], D, FFN, 512)
        precast("wd%d" % l, ffn_wd[l], FFN, D, 512)

    consts = sb("consts", [P, C_RM + T])
    sc.dma("sp", consts.t[:], consts_in, writes=[consts])
    cbf = sb("cbf", [P, 640], BF16)
    sc.op("dve", lambda h: h.tensor_copy(out=cbf.t[:], in_=consts.t[:, 0:640]), [consts], [cbf])
    ident_b = cbf.t[:, C_ID:C_ID + P]
    ones_b = cbf.t[:, C_ONES:C_ONES + P]
    ident_f = consts.t[:, C_ID:C_ID + P]
    caus_f = consts.t[:, C_CAUS:C_CAUS + P]
    lst_f = consts.t[:, C_LST:C_LST + P]
    esel_f = consts.t[:, C_ESEL:C_ESEL + P]
    rmask = consts.t[:, C_RM:C_RM + T]

    gv = sb("gv", [P, 14, KC])
    for l in range(4):
        sc.dma("sp", gv.t[:, l, :], g_mix[l], writes=[gv])
        sc.dma("sp", gv.t[:, 4 + l, :], g_xat[l], writes=[gv])
        sc.dma("sp", gv.t[:, 8 + l, :], g_ffn[l], writes=[gv])
    sc.dma("sp", gv.t[:, 12, :], g_mem, writes=[gv])
    sc.dma("sp", gv.t[:, 13, :], g_fin, writes=[gv])

    xT = sb("xT", [P, KC, T])
    hT = sb("hT", [P, KC, T], BF16)
    sq = sb("sq", [P, KC, T], BF16)
    rstd = sb("rstd", [P, T])
    wb = [sb("wb%d" % i, [P, 8192], BF16) for i in range(3)]
    wstate = {"i": 0}
    big1 = sb("big1", [P, 48 * T], BF16)
    big2 = sb("big2", [P, 32 * T], BF16)
    tmp = [sb("tmp%d" % i, [P, 4 * P]) for i in range(6)]
    tstate = {"i": 0}

    def tmpf():
        b = tmp[tstate["i"] % len(tmp)]
        tstate["i"] += 1
        return b

    psb = [Buf(es.enter_context(nc.psum_tensor("ps%d" % i, [P, 512], F32)), "ps%d" % i, excl=True) for i in range(8)]
    pstate = {"i": 0}

    def psum():
        b = psb[pstate["i"] % 6]
        pstate["i"] += 1
        return b

    memT = sb("memT", [P, KC, MEM], BF16)
    KT = sb("KT", [P, KC, MEM], BF16)
    Vm = sb("Vm", [P, 2, D], BF16)

    def wload(key, kc, c0, ncols, kc0=0):
        b = wb[wstate["i"] % 3]
        wstate["i"] += 1
        src, bc = wbf[key]
        assert bc == ncols and c0 % ncols == 0, (key, bc, ncols, c0)
        view = b.t[:, 0:kc * ncols].rearrange("p (c n) -> p c n", n=ncols)
        sc.dma("sp", view, src.t[c0 // ncols][:, kc0:kc0 + kc, :], reads=[src], writes=[b])
        return b, view

    def rmsnorm_T(src, gidx, n, dst):
        sc.op("act", lambda h: h.activation(out=sq.t[:, :, 0:n], in_=src.t[:, :, 0:n], func=AF.Square), [src], [sq])
        ps = psum()

        def mmf(h):
            ins = None
            for c in range(KC):
                ins = h.matmul(ps.t[:, 0:n], lhsT=ones_b, rhs=sq.t[:, c, 0:n], start=(c == 0), stop=(c == KC - 1))
            return ins
        sc.op("pe", mmf, [sq, cbf], [ps])
        sc.op("act", lambda h: h.activation(out=rstd.t[:, 0:n], in_=ps.t[:, 0:n], func=AF.Sqrt, scale=1.0 / D, bias=1e-6),
              [ps], [rstd])
        sc.op("dve", lambda h: h.reciprocal(out=rstd.t[:, 0:n], in_=rstd.t[:, 0:n]), [rstd], [rstd])

        def nf(h):
            ins = None
            for c in range(KC):
                ins = h.scalar_tensor_tensor(out=dst.t[:, c, 0:n], in0=src.t[:, c, 0:n], scalar=gv.t[:, gidx, c:c + 1],
                                             in1=rstd.t[:, 0:n], op0=ALU.mult, op1=ALU.mult)
            return ins
        sc.op("dve", nf, [src, rstd, gv], [dst])

    def fm_proj(key, c0, nchunks, rhs, kc, n, consume, blk=4):
        j = 0
        while j < nchunks:
            nb = min(blk, nchunks - j)
            wbuf_, wv = wload(key, kc, c0 + j * P, nb * P)
            for jj in range(nb):
                ps = psum()

                def mmf(h, jj=jj, ps=ps, wv=wv):
                    ins = None
                    for c in range(kc):
                        ins = h.matmul(ps.t[:, 0:n], lhsT=wv[:, c, jj * P:(jj + 1) * P], rhs=rhs.t[:, c, 0:n],
                                       start=(c == 0), stop=(c == kc - 1))
                    return ins
                sc.op("pe", mmf, [wbuf_, rhs], [ps])
                consume(j + jj, ps)
            j += nb

    def tm_proj(key, c0, ncols, lhs, consume):
        for b0 in range(0, ncols, 512):
            nb = min(512, ncols - b0)
            wbuf_, wv = wload(key, KC, c0 + b0, nb)
            for ch in range(NCH):
                ps = psum()

                def mmf(h, ch=ch, ps=ps, wv=wv, nb=nb):
                    ins = None
                    for c in range(KC):
                        ins = h.matmul(ps.t[:, 0:nb], lhsT=lhs.t[:, c, ch * P:(ch + 1) * P], rhs=wv[:, c, 0:nb],
                                       start=(c == 0), stop=(c == KC - 1))
                    return ins
                sc.op("pe", mmf, [wbuf_, lhs], [ps])
                consume(b0, nb, ch, ps)

    def resid_add(j, ps):
        sc.op("dve", lambda h: h.tensor_tensor(out=xT.t[:, j, :], in0=xT.t[:, j, :], in1=ps.t[:, 0:T], op=ALU.add),
              [xT, ps], [xT])

    def as_proj(key, lhs, kct, consume):
        for b0 in range(0, D, 512):
            pss_ = [psum() for _ in range(NCH)]
            for k0 in range(0, kct, KC):
                kn = min(KC, kct - k0)
                wbuf_, wv = wload(key, kn, b0, 512, kc0=k0)
                for ch in range(NCH):
                    def mmf(h, ch=ch, ps=pss_[ch], wv=wv, k0=k0, kn=kn):
                        ins = None
                        for c in range(kn):
                            ins = h.matmul(ps.t[:, 0:512], lhsT=lhs.t[:, k0 + c, ch * P:(ch + 1) * P], rhs=wv[:, c, 0:512],
                                           start=(k0 + c == 0), stop=(k0 + c == kct - 1))
                        return ins
                    sc.op("pe", mmf, [wbuf_, lhs], [pss_[ch]])
            for ch in range(NCH):
                consume(b0, ch, pss_[ch])

    def resid_add_tm(b0, ch, ps):
        t_ = tmpf()
        sc.op("act", lambda h: h.activation(out=t_.t[:], in_=ps.t[:, 0:512], func=AF.Copy), [ps], [t_])
        p2 = psum()

        def tf(h):
            ins = None
            for i in range(4):
                ins = h.transpose(out=p2.t[:, i * P:(i + 1) * P], in_=t_.t[:, i * P:(i + 1) * P], identity=ident_f)
            return ins
        sc.op("pe", tf, [t_, consts], [p2])
        j0 = b0 // P
        xv = xT.t[:, j0:j0 + 4, ch * P:(ch + 1) * P]
        sc.op("dve", lambda h: h.tensor_tensor(out=xv, in0=xv, in1=p2.t[:].rearrange("p (c t) -> p c t", t=P), op=ALU.add),
              [xT, p2], [xT])

    def transpose_to(dst_ap_fn, src_aps, dst_buf, src_bufs, evac_eng="act"):
        ps = psum()
        pv = ps.t[:].bitcast(BF16)
        n = len(src_aps)

        def tf(h):
            ins = None
            for i, a in enumerate(src_aps):
                ins = h.transpose(out=pv[:, i * P:(i + 1) * P], in_=a, identity=ident_b)
            return ins
        sc.op("pe", tf, list(src_bufs) + [cbf], [ps])
        dst = dst_ap_fn(n)
        srcv = pv[:, 0:n * P]
        if len(dst.shape) == 3:
            srcv = srcv.rearrange("p (c t) -> p c t", t=P)
        sc.op(evac_eng, (lambda h: h.activation(out=dst, in_=srcv, func=AF.Copy)) if evac_eng == "act"
              else (lambda h: h.tensor_copy(out=dst, in_=srcv)), [ps], [dst_buf])

    for mp in range(MEM // P):
        sc.dma("sp", xT.t[:, :, 0:P], memT_in.rearrange("c p m -> p c m")[:, :, mp * P:(mp + 1) * P], writes=[xT])
        rmsnorm_T(xT, 12, P, hT)
        sc.op("dve", lambda h, mp=mp: h.tensor_copy(out=memT.t[:, :, mp * P:(mp + 1) * P], in_=hT.t[:, :, 0:P]), [hT], [memT])

    ARENA = 17000
    arena_t = es.enter_context(nc.sbuf_tensor("arena", [P, ARENA], F32))
    astate = {"off": 0}

    def ar(name, shape, dt=F32):
        n = int(np.prod(shape[1:]))
        words = n if dt == F32 else (n + 1) // 2
        off = astate["off"]
        assert off + words <= ARENA, (name, off, words)
        astate["off"] = off + words
        v = arena_t[:, off:off + words]
        if dt != F32:
            v = v.bitcast(BF16)[:, 0:n]
        if len(shape) == 3:
            v = v.rearrange("p (a b) -> p a b", b=shape[2])
        return Buf(v, name)

    ev = None
    od = None
    if any(l % 2 == 0 for l in layers):
        astate["off"] = 0
        ev = dict(
            cw=ar("e_cw", [P, 8, 4]), cb=ar("e_cb", [P, 8]), br=ar("e_br", [P, 8]), bi=ar("e_bi", [P, 8]),
            lam=ar("e_lam", [P, 8]), n8=ar("e_n8", [P, 8]), n16=ar("e_n16", [P, 8]),
            lb=ar("e_lb", [P, 8]), oml=ar("e_oml", [P, 8]), lbt=ar("e_lbt", [P, 2, 8]),
            hn=ar("e_hn", [P, 1024]), wr=ar("e_wr", [P, 8, P], BF16), wi=ar("e_wi", [P, 8, P], BF16),
            wst=ar("e_wst", [P, 8, P]),
            hist=ar("e_hist", [P, 8, 3]), hst=ar("e_hst", [P, 8]),
            S=ar("e_S", [P, 8, P]), Sb=ar("e_Sb", [P, 8, P], BF16),
            xr=ar("e_xr", [P, T + 3]), xc=ar("e_xc", [P, T]), xcb=ar("e_xcb", [P, T], BF16),
            qs=ar("e_qs", [P, 8, T]), kk=ar("e_kk", [P, 8, T]), cum=ar("e_cum", [P, 8, T]), lf=ar("e_lf", [P, 8, T]),
            cref=ar("e_cref", [P, 8, T // 32]), Qt=ar("e_Qt", [P, 8, T], BF16), Qp=ar("e_Qp", [P, 8, T], BF16),
            Kt=ar("e_Kt", [P, 4, P], BF16), dk=ar("e_dk", [P, 4, P]), Kp=ar("e_Kp", [P, P], BF16),
            KpT=ar("e_KpT", [P, 8, P], BF16), eC=ar("e_eC", [P, 8]),
            ivt=ar("e_ivt", [P, NCH, 1024], BF16), gbt=ar("e_gbt", [P, NCH, 1024], BF16),
            Am=ar("e_Am", [P, 8, P], BF16), o=ar("e_o", [P, 1024]), ob=ar("e_ob", [P, 1024], BF16),
            ss=ar("e_ss", [P, 8]),
        )
    if any(l % 2 == 1 for l in layers):
        astate["off"] = 0
        od = dict(
            cw=ar("o_cw", [P, 48, 4]), cb=ar("o_cb", [P, 48]), dtb=ar("o_dtb", [P, 1]), A=ar("o_A", [P, 1]),
            dsk=ar("o_dsk", [P, 64]), nwT=ar("o_nwT", [P, 32]),
            hist=ar("o_hist", [P, 48, 3]), xr=ar("o_xr", [P, T + 3]), xc=ar("o_xc", [P, T]),
            wdt=ar("o_wdt", [P, KC, P], BF16),
            sz=ar("o_sz", [P, NCH, 4096], BF16), dtT=ar("o_dtT", [P, T]),
            dtk=ar("o_dtk", [P, P]), cumk=ar("o_cumk", [P, 64]), Ek=ar("o_Ek", [P, 64]), wend=ar("o_wend", [P, 64]),
            ecC=ar("o_ecC", [P, 64]), R=ar("o_R", [P, 8, P]), dec=ar("o_dec", [P, 8, P], BF16),
            CBm=ar("o_CBm", [P, P], BF16), xst=ar("o_xst", [P, 512], BF16), Xt=ar("o_Xt", [P, 512], BF16),
            Xe=ar("o_Xe", [P, 512], BF16), Y=ar("o_Y", [P, 512]), Yn=ar("o_Yn", [P, 512], BF16),
            st=ar("o_st", [P, 4096]), stb=ar("o_stb", [P, 4096], BF16), Bt=ar("o_Bt", [P, 8, P], BF16),
            ss=ar("o_ss", [P, 1]),
        )

    def even_prep(l):
        e = l // 2
        d = ev
        for k_, src in (("cw", lru_cw[e]), ("cb", lru_cb[e]), ("br", lru_br[e]), ("bi", lru_bi[e]), ("lam", lru_lam[e])):
            sc.dma("sp", d[k_].t[:], src, writes=[d[k_]])
        sc.dma("sp", d["lbt"].t[:, 0, :], hg_lb[0], writes=[d["lbt"]])
        sc.dma("sp", d["lbt"].t[:, 1, :], hg_lb[1], writes=[d["lbt"]])
        sc.dma("sp", d["hn"].t[:], hg_norm[e].partition_broadcast(P), writes=[d["hn"]])
        for nm, src in (("wr", lru_wr[e]), ("wi", lru_wi[e])):
            sc.dma("sp", d["wst"].t[:], src.rearrange("h i j -> i h j"), writes=[d["wst"]])
            sc.op("dve", lambda h, nm=nm: h.tensor_copy(out=d[nm].t[:], in_=d["wst"].t[:]), [d["wst"]], [d[nm]])
        sc.op("act", lambda h: h.activation(out=d["n8"].t[:], in_=d["lam"].t[:], func=AF.Exp, scale=-1.0), [d["lam"]], [d["n8"]])
        sc.op("act", lambda h: h.activation(out=d["n8"].t[:], in_=d["n8"].t[:], func=AF.Ln, bias=1.0), [d["n8"]], [d["n8"]])
        sc.op("dve", lambda h: h.tensor_scalar(out=d["n16"].t[:], in0=d["n8"].t[:], scalar1=-16.0, scalar2=None, op0=ALU.mult),
              [d["n8"]], [d["n16"]])
        sc.op("dve", lambda h: h.tensor_scalar(out=d["n8"].t[:], in0=d["n8"].t[:], scalar1=-8.0, scalar2=None, op0=ALU.mult),
              [d["n8"]], [d["n8"]])
        if e == 0:
            sc.op("dve", lambda h: h.memset(d["lb"].t[:], 0.0), [], [d["lb"]])
        else:
            sc.op("dve", lambda h: h.tensor_tensor(out=d["lb"].t[:], in0=d["lbt"].t[:, 1, :], in1=d["lbt"].t[:, 0, :], op=ALU.subtract),
                  [d["lbt"]], [d["lb"]])
            sc.op("act", lambda h: h.activation(out=d["lb"].t[:], in_=d["lb"].t[:], func=AF.Sigmoid), [d["lb"]], [d["lb"]])
        sc.op("dve", lambda h: h.tensor_scalar(out=d["oml"].t[:], in0=d["lb"].t[:], scalar1=-1.0, scalar2=1.0, op0=ALU.mult, op1=ALU.add),
              [d["lb"]], [d["oml"]])
        sc.op("dve", lambda h: h.memset(d["hist"].t[:], 0.0), [], [d["hist"]])
        sc.op("dve", lambda h: h.memset(d["hst"].t[:], 0.0), [], [d["hst"]])
        sc.op("dve", lambda h: h.memset(d["S"].t[:], 0.0), [], [d["S"]])
        sc.op("dve", lambda h: h.memset(d["Sb"].t[:], 0.0), [], [d["Sb"]])
        sc.op("dve", lambda h: h.memset(d["cref"].t[:], 0.0), [], [d["cref"]])

    if ev is not None:
        ev["hall"] = ar("e_hall", [P, 8, T])

    def even_layer_tile(l, first_tile):
        d = ev
        key = "min%d" % l
        yT = big2
        yv = yT.t[:, 0:KC * T].rearrange("p (c t) -> p c t", t=T)

        def xa_consume(c, ps):
            xr, xc, xcb = d["xr"], d["xc"], d["xcb"]
            sc.op("act", lambda h: h.activation(out=xr.t[:, 3:3 + T], in_=ps.t[:, 0:T], func=AF.Copy), [ps], [xr])
            sc.op("dve", lambda h: h.tensor_copy(out=xr.t[:, 0:3], in_=d["hist"].t[:, c, :]), [d["hist"]], [xr])
            sc.op("dve", lambda h: h.tensor_copy(out=d["hist"].t[:, c, :], in_=xr.t[:, T:T + 3]), [xr], [d["hist"]])
            sc.op("act", lambda h: h.activation(out=xc.t[:], in_=xr.t[:, 3:3 + T], func=AF.Identity,
                                                scale=d["cw"].t[:, c, 3:4], bias=d["cb"].t[:, c:c + 1]), [xr, d["cw"], d["cb"]], [xc])

            for k in range(3):
                sc.op("dve", lambda h, k=k: h.scalar_tensor_tensor(out=xc.t[:], in0=xr.t[:, k:k + T], scalar=d["cw"].t[:, c, k:k + 1],
                                                                  in1=xc.t[:], op0=ALU.mult, op1=ALU.add), [xr, d["cw"], xc], [xc])
            sc.op("act", lambda h: h.activation(out=xcb.t[:], in_=xc.t[:], func=AF.Copy), [xc], [xcb])
            pr, pi = psum(), psum()
            sc.op("pe", lambda h: h.matmul(pr.t[:, 0:T], lhsT=d["wr"].t[:, c, :], rhs=xcb.t[:], start=True, stop=True), [d["wr"], xcb], [pr])
            sc.op("pe", lambda h: h.matmul(pi.t[:, 0:T], lhsT=d["wi"].t[:, c, :], rhs=xcb.t[:], start=True, stop=True), [d["wi"], xcb], [pi])
            r, ig, a, m = tmpf(), tmpf(), tmpf(), tmpf()
            sc.op("act", lambda h: h.activation(out=r.t[:, 0:T], in_=pr.t[:, 0:T], func=AF.Sigmoid, bias=d["br"].t[:, c:c + 1]), [pr, d["br"]], [r])
            sc.op("act", lambda h: h.activation(out=ig.t[:, 0:T], in_=pi.t[:, 0:T], func=AF.Sigmoid, bias=d["bi"].t[:, c:c + 1]), [pi, d["bi"]], [ig])
            sc.op("act", lambda h: h.activation(out=a.t[:, 0:T], in_=r.t[:, 0:T], func=AF.Exp, scale=d["n8"].t[:, c:c + 1]), [r, d["n8"]], [a])
            sc.op("act", lambda h: h.activation(out=m.t[:, 0:T], in_=r.t[:, 0:T], func=AF.Exp, scale=d["n16"].t[:, c:c + 1]), [r, d["n16"]], [m])
            sc.op("act", lambda h: h.activation(out=m.t[:, 0:T], in_=m.t[:, 0:T], func=AF.Sqrt, scale=-1.0, bias=1.0), [m], [m])

            sc.op("dve", lambda h: h.tensor_tensor(out=m.t[:, 0:T], in0=m.t[:, 0:T], in1=ig.t[:, 0:T], op=ALU.mult), [m, ig], [m])
            sc.op("dve", lambda h: h.tensor_tensor(out=m.t[:, 0:T], in0=m.t[:, 0:T], in1=xc.t[:], op=ALU.mult), [m, xc], [m])
            sc.op("dve", lambda h: h.tensor_tensor_scan(out=d["hall"].t[:, c, :], data0=a.t[:, 0:T], data1=m.t[:, 0:T],
                                                        initial=d["hst"].t[:, c:c + 1], op0=ALU.mult, op1=ALU.add), [a, m, d["hst"]], [d["hall"]])
            sc.op("dve", lambda h: h.tensor_copy(out=d["hst"].t[:, c:c + 1], in_=d["hall"].t[:, c, T - 1:T]), [d["hall"]], [d["hst"]])

        fm_proj(key, 0, 8, hT, KC, T, xa_consume)

        def ga_consume(c, ps):
            g = tmpf()
            sc.op("act", lambda h: h.activation(out=g.t[:, 0:T], in_=ps.t[:, 0:T], func=AF.Gelu_apprx_tanh), [ps], [g])
            sc.op("dve", lambda h: h.tensor_tensor(out=yv[:, c, :], in0=g.t[:, 0:T], in1=d["hall"].t[:, c, :], op=ALU.mult),
                  [g, d["hall"]], [yT])
        fm_proj(key, 1024, 8, hT, KC, T, ga_consume)

        def q_consume(hd, ps):
            sc.op("act", lambda h: h.activation(out=d["qs"].t[:, hd, :], in_=ps.t[:, 0:T], func=AF.Silu), [ps], [d["qs"]])
        fm_proj(key, 2048, 8, hT, KC, T, q_consume)

        def f_consume(hd, ps):
            sg = tmpf()
            sc.op("act", lambda h: h.activation(out=sg.t[:, 0:T], in_=ps.t[:, 0:T], func=AF.Sigmoid), [ps], [sg])
            sc.op("dve", lambda h: h.tensor_scalar(out=sg.t[:, 0:T], in0=sg.t[:, 0:T], scalar1=d["oml"].t[:, hd:hd + 1],
                                                   scalar2=d["lb"].t[:, hd:hd + 1], op0=ALU.mult, op1=ALU.add), [sg, d["oml"], d["lb"]], [sg])
            sc.op("dve", lambda h: h.tensor_scalar(out=d["kk"].t[:, hd, :], in0=sg.t[:, 0:T], scalar1=-1.0, scalar2=1.0,
                                                   op0=ALU.mult, op1=ALU.add), [sg], [d["kk"]])
            sc.op("act", lambda h: h.activation(out=d["lf"].t[:, hd, :], in_=sg.t[:, 0:T], func=AF.Ln), [sg], [d["lf"]])
        fm_proj(key, 3072, 8, hT, KC, T, f_consume)

        def iv_consume(b0, nb, ch, ps):
            sc.op("act", lambda h: h.activation(out=d["ivt"].t[:, ch, b0:b0 + nb], in_=ps.t[:, 0:nb], func=AF.Copy), [ps], [d["ivt"]])
        tm_proj(key, 4096, 1024, hT, iv_consume)

        def gb_consume(b0, nb, ch, ps):
            sc.op("act", lambda h: h.activation(out=d["gbt"].t[:, ch, b0:b0 + nb], in_=ps.t[:, 0:nb], func=AF.Silu), [ps], [d["gbt"]])
        tm_proj(key, 5120, 1024, hT, gb_consume)

        def scanf(h):
            ins = None
            for hd in range(8):
                ins = h.tensor_tensor_scan(out=d["cum"].t[:, hd, :], data0=rmask, data1=d["lf"].t[:, hd, :], initial=0.0,
                                           op0=ALU.mult, op1=ALU.add)
            return ins
        sc.op("dve", scanf, [d["lf"], consts], [d["cum"]])
        NB = T // 32
        cumv = d["cum"].t[:].rearrange("p h (b k) -> p h b k", k=32)

        def creff(h):
            ins = None
            for ch in range(NCH):
                ins = h.tensor_copy(out=d["cref"].t[:, :, 4 * ch + 1:4 * ch + 4], in_=cumv[:, :, 4 * ch:4 * ch + 3, 31])
            return ins
        sc.op("dve", creff, [d["cum"]], [d["cref"]])
        dq = d["hall"]
        dqv = dq.t[:].rearrange("p h (b k) -> p h b k", k=32)
        sc.op("dve", lambda h: h.tensor_tensor(out=dqv, in0=cumv, in1=bcast(d["cref"].t[:], 3, 32), op=ALU.subtract),
              [d["cum"], d["cref"]], [dq])
        sc.op("act", lambda h: h.activation(out=dq.t[:], in_=dq.t[:], func=AF.Exp), [dq], [dq])
        sc.op("dve", lambda h: h.tensor_tensor(out=d["Qt"].t[:], in0=dq.t[:], in1=d["qs"].t[:], op=ALU.mult), [dq, d["qs"]], [d["Qt"]])
        sc.op("act", lambda h: h.activation(out=dq.t[:], in_=d["cum"].t[:], func=AF.Exp), [d["cum"]], [dq])
        sc.op("dve", lambda h: h.tensor_tensor(out=d["Qp"].t[:], in0=dq.t[:], in1=d["qs"].t[:], op=ALU.mult), [dq, d["qs"]], [d["Qp"]])

        for ch in range(NCH):
            c0 = ch * P
            pso = [psb[6], psb[7]]
            for hd in range(8):
                sc.op("dve", lambda h, hd=hd: h.tensor_tensor(out=d["dk"].t[:], in0=bcast(d["cref"].t[:, hd, 4 * ch:4 * ch + 4], 2, P),
                                                             in1=bcast(d["cum"].t[:, hd, c0:c0 + P], 1, 4), op=ALU.subtract),
                      [d["cref"], d["cum"]], [d["dk"]])
                sc.op("pool", lambda h: h.tensor_scalar(out=d["dk"].t[:], in0=d["dk"].t[:], scalar1=60.0, scalar2=None, op0=ALU.min),
                      [d["dk"]], [d["dk"]])
                sc.op("act", lambda h: h.activation(out=d["dk"].t[:], in_=d["dk"].t[:], func=AF.Exp), [d["dk"]], [d["dk"]])
                sc.op("dve", lambda h, hd=hd: h.tensor_tensor(out=d["Kt"].t[:], in0=d["dk"].t[:],
                                                             in1=bcast(d["kk"].t[:, hd, c0:c0 + P], 1, 4), op=ALU.mult),
                      [d["dk"], d["kk"]], [d["Kt"]])
                psA = psum()

                def af(h, hd=hd, psA=psA):
                    ins = None
                    for i in range(4):
                        ins = h.matmul(psA.t[:, 32 * i:32 * i + 32], lhsT=d["Kt"].t[:, i, :],
                                       rhs=d["Qt"].t[:, hd, c0 + 32 * i:c0 + 32 * i + 32], start=True, stop=True)
                    return ins
                sc.op("pe", af, [d["Kt"], d["Qt"]], [psA])
                sc.op("dve", lambda h, hd=hd, psA=psA: h.tensor_tensor(out=d["Am"].t[:, hd, :], in0=psA.t[:, 0:P], in1=caus_f, op=ALU.mult),
                      [psA, consts], [d["Am"]])
                ec = tmpf()
                sc.op("act", lambda h, hd=hd, ec=ec: h.activation(out=ec.t[:, 0:P], in_=d["cum"].t[:, hd, c0:c0 + P], func=AF.Exp, scale=-1.0,
                                                                  bias=d["cum"].t[:, hd, c0 + P - 1:c0 + P]), [d["cum"]], [ec])
                sc.op("dve", lambda h, hd=hd, ec=ec: h.tensor_tensor(out=d["Kp"].t[:], in0=ec.t[:, 0:P], in1=d["kk"].t[:, hd, c0:c0 + P], op=ALU.mult),
                      [ec, d["kk"]], [d["Kp"]])
                transpose_to(lambda n, hd=hd: d["KpT"].t[:, hd, :], [d["Kp"].t[:]], d["KpT"], [d["Kp"]])
                sc.op("act", lambda h, hd=hd: h.activation(out=d["eC"].t[:, hd:hd + 1], in_=d["cum"].t[:, hd, c0 + P - 1:c0 + P], func=AF.Exp),
                      [d["cum"]], [d["eC"]])
                po = pso[hd // 4]
                oc = (hd % 4) * P

                def of(h, hd=hd, po=po, oc=oc):
                    h.matmul(po.t[:, oc:oc + P], lhsT=d["Am"].t[:, hd, :], rhs=d["ivt"].t[:, ch, hd * P:(hd + 1) * P], start=True, stop=False)
                    return h.matmul(po.t[:, oc:oc + P], lhsT=d["Qp"].t[:, hd, c0:c0 + P], rhs=d["Sb"].t[:, hd, :], start=False, stop=True)
                sc.op("pe", of, [d["Am"], d["ivt"], d["Qp"], d["Sb"]], [po])
                pS = psum()
                sc.op("pe", lambda h, hd=hd, pS=pS: h.matmul(pS.t[:, 0:P], lhsT=d["KpT"].t[:, hd, :], rhs=d["ivt"].t[:, ch, hd * P:(hd + 1) * P],
                                                           start=True, stop=True), [d["KpT"], d["ivt"]], [pS])
                sc.op("dve", lambda h, hd=hd, pS=pS: h.scalar_tensor_tensor(out=d["S"].t[:, hd, :], in0=d["S"].t[:, hd, :], scalar=d["eC"].t[:, hd:hd + 1],
                                                                          in1=pS.t[:, 0:P], op0=ALU.mult, op1=ALU.add), [d["S"], d["eC"], pS], [d["S"]])
                sc.op("act", lambda h, hd=hd: h.activation(out=d["Sb"].t[:, hd, :], in_=d["S"].t[:, hd, :], func=AF.Copy), [d["S"]], [d["Sb"]])
            o = d["o"]
            sc.op("act", lambda h: h.activation(out=o.t[:, 0:512], in_=pso[0].t[:, 0:512], func=AF.Copy), [pso[0]], [o])
            sc.op("act", lambda h: h.activation(out=o.t[:, 512:1024], in_=pso[1].t[:, 0:512], func=AF.Copy), [pso[1]], [o])
            osq = tmpf(), tmpf()
            sc.op("dve", lambda h: h.tensor_tensor(out=osq[0].t[:], in0=o.t[:, 0:512], in1=o.t[:, 0:512], op=ALU.mult), [o], [osq[0]])
            sc.op("dve", lambda h: h.tensor_tensor(out=osq[1].t[:], in0=o.t[:, 512:1024], in1=o.t[:, 512:1024], op=ALU.mult), [o], [osq[1]])
            sc.op("dve", lambda h: h.tensor_reduce(out=d["ss"].t[:, 0:4], in_=osq[0].t[:].rearrange("p (h v) -> p h v", v=P), axis=AX.X, op=ALU.add),
                  [osq[0]], [d["ss"]])
            sc.op("dve", lambda h: h.tensor_reduce(out=d["ss"].t[:, 4:8], in_=osq[1].t[:].rearrange("p (h v) -> p h v", v=P), axis=AX.X, op=ALU.add),
                  [osq[1]], [d["ss"]])
            sc.op("act", lambda h: h.activation(out=d["ss"].t[:], in_=d["ss"].t[:], func=AF.Sqrt, scale=1.0 / P, bias=1e-6), [d["ss"]], [d["ss"]])
            sc.op("dve", lambda h: h.reciprocal(out=d["ss"].t[:], in_=d["ss"].t[:]), [d["ss"]], [d["ss"]])
            ov = o.t[:].rearrange("p (h v) -> p h v", v=P)
            sc.op("dve", lambda h: h.tensor_tensor(out=ov, in0=ov, in1=bcast(d["ss"].t[:], 2, P), op=ALU.mult), [o, d["ss"]], [o])
            sc.op("dve", lambda h: h.tensor_tensor(out=o.t[:], in0=o.t[:], in1=d["hn"].t[:], op=ALU.mult), [o, d["hn"]], [o])
            sc.op("dve", lambda h: h.tensor_tensor(out=d["ob"].t[:], in0=o.t[:], in1=d["gbt"].t[:, ch, :], op=ALU.mult), [o, d["gbt"]], [d["ob"]])
            transpose_to(lambda n: yv[:, 8:16, c0:c0 + P], [d["ob"].t[:, hd * P:(hd + 1) * P] for hd in range(8)], yT, [d["ob"]],
                         evac_eng="dve")
        if DBG == "a":
            sc.op("dve", lambda h: h.memset(yv[:, 8:16, :], 0.0), [], [yT])
        if DBG == "b":
            sc.op("dve", lambda h: h.memset(yv[:, 0:8, :], 0.0), [], [yT])
        as_proj("mout%d" % l, yT_view(yT, KC), KC, resid_add_tm)

    def yT_view(buf, nchunks):
        return ViewBuf(buf, buf.t[:, 0:nchunks * T].rearrange("p (c t) -> p c t", t=T))

    def xattn_prep(l):
        def k_consume(j, ps):
            sc.op("act", lambda h: h.activation(out=KT.t[:, j, :], in_=ps.t[:, 0:MEM], func=AF.Copy), [ps], [KT])
        fm_proj("wkv%d" % l, 0, KC, memT, KC, MEM, k_consume)
        for b0 in range(0, D, 512):
            wbuf_, wv = wload("wkv%d" % l, KC, D + b0, 512)
            for mc in range(2):
                ps = psum()

                def mmf(h, mc=mc, ps=ps, wv=wv):
                    ins = None
                    for c in range(KC):
                        ins = h.matmul(ps.t[:, 0:512], lhsT=memT.t[:, c, mc * P:(mc + 1) * P], rhs=wv[:, c, 0:512],
                                       start=(c == 0), stop=(c == KC - 1))
                    return ins
                sc.op("pe", mmf, [wbuf_, memT], [ps])
                sc.op("act", lambda h, mc=mc, ps=ps, b0=b0: h.activation(out=Vm.t[:, mc, b0:b0 + 512], in_=ps.t[:, 0:512], func=AF.Copy), [ps], [Vm])

    if True:
        at = dict(p=sb("a_p", [P, 4, MEM]), pb=sb("a_pb", [P, 4, MEM], BF16), pT=sb("a_pT", [P, 8, P], BF16),
                  mx=sb("a_mx", [P, 4]), sm=sb("a_sm", [P, 4]))

    def xattn_tile(l):
        d = at
        qT = ViewBuf(big2, big2.t[:, KC * T:2 * KC * T].rearrange("p (c t) -> p c t", t=T))
        oT = ViewBuf(big2, big2.t[:, 0:KC * T].rearrange("p (c t) -> p c t", t=T))
        rmsnorm_T(xT, 4 + l, T, hT)

        def q_consume(j, ps):
            sc.op("act", lambda h: h.activation(out=qT.t[:, j, :], in_=ps.t[:, 0:T], func=AF.Copy), [ps], [big2])
        fm_proj("wq%d" % l, 0, KC, hT, KC, T, q_consume)
        scale = 512.0 ** -0.5
        for ch in range(NCH):
            c0 = ch * P
            pss = [psum(), psum()]
            for a in range(4):
                ps = pss[a // 2]
                oc = (a % 2) * MEM

                def sf(h, a=a, ps=ps, oc=oc):
                    ins = None
                    for dc in range(4):
                        ins = h.matmul(ps.t[:, oc:oc + MEM], lhsT=qT.t[:, 4 * a + dc, c0:c0 + P], rhs=KT.t[:, 4 * a + dc, :],
                                       start=(dc == 0), stop=(dc == 3))
                    return ins
                sc.op("pe", sf, [big2, KT], [ps])
            for hh in range(2):
                sc.op("dve", lambda h, hh=hh: h.tensor_reduce(out=d["mx"].t[:, 2 * hh:2 * hh + 2],
                                                             in_=pss[hh].t[:].rearrange("p (a m) -> p a m", m=MEM), axis=AX.X, op=ALU.max),
                      [pss[hh]], [d["mx"]])
            sc.op("dve", lambda h: h.tensor_scalar(out=d["mx"].t[:], in0=d["mx"].t[:], scalar1=-scale, scalar2=None, op0=ALU.mult), [d["mx"]], [d["mx"]])
            for a in range(4):
                ps = pss[a // 2]
                oc = (a % 2) * MEM
                sc.op("act", lambda h, a=a, ps=ps, oc=oc: h.activation(out=d["p"].t[:, a, :], in_=ps.t[:, oc:oc + MEM], func=AF.Exp, scale=scale,
                                                                     bias=d["mx"].t[:, a:a + 1], accum_out=d["sm"].t[:, a:a + 1]),
                      [ps, d["mx"]], [d["p"], d["sm"]])
            sc.op("dve", lambda h: h.reciprocal(out=d["sm"].t[:], in_=d["sm"].t[:]), [d["sm"]], [d["sm"]])
            sc.op("dve", lambda h: h.tensor_tensor(out=d["pb"].t[:], in0=d["p"].t[:], in1=bcast(d["sm"].t[:], 2, MEM), op=ALU.mult),
                  [d["p"], d["sm"]], [d["pb"]])
            transpose_to(lambda n: d["pT"].t[:], [d["pb"].t[:, a, mc * P:(mc + 1) * P] for a in range(4) for mc in range(2)],
                         d["pT"], [d["pb"]])
            for a in range(4):
                ps = psum()

                def of(h, a=a, ps=ps):
                    ins = None
                    for dc in range(4):
                        for mc in range(2):
                            ins = h.matmul(ps.t[:, dc * P:(dc + 1) * P], lhsT=Vm.t[:, mc, (4 * a + dc) * P:(4 * a + dc + 1) * P],
                                           rhs=d["pT"].t[:, 2 * a + mc, :], start=(mc == 0), stop=(mc == 1))
                    return ins
                sc.op("pe", of, [Vm, d["pT"]], [ps])
                sc.op("act", lambda h, a=a, ps=ps: h.activation(out=oT.t[:, 4 * a:4 * a + 4, c0:c0 + P],
                                                              in_=ps.t[:].rearrange("p (c t) -> p c t", t=P), func=AF.Copy), [ps], [big2])
        as_proj("wo%d" % l, oT, KC, resid_add_tm)

    abuf = [sb("abuf%d" % i, [P, 512], BF16) for i in range(2)]
    abstate = {"i": 0}

    def ffn_tile(l):
        rmsnorm_T(xT, 8 + l, T, hT)
        av = ViewBuf(big1, big1.t[:, 0:FC * T].rearrange("p (c t) -> p c t", t=T))
        for b0 in range(0, FFN, 512):
            wg_, wgv = wload("wg%d" % l, KC, b0, 512)
            wu_, wuv = wload("wu%d" % l, KC, b0, 512)
            for ch in range(NCH):
                pg, pu = psum(), psum()

                def mg(h, ch=ch, pg=pg, wgv=wgv):
                    ins = None
                    for c in range(KC):
                        ins = h.matmul(pg.t[:, 0:512], lhsT=hT.t[:, c, ch * P:(ch + 1) * P], rhs=wgv[:, c, 0:512], start=(c == 0), stop=(c == KC - 1))
                    return ins

                def mu(h, ch=ch, pu=pu, wuv=wuv):
                    ins = None
                    for c in range(KC):
                        ins = h.matmul(pu.t[:, 0:512], lhsT=hT.t[:, c, ch * P:(ch + 1) * P], rhs=wuv[:, c, 0:512], start=(c == 0), stop=(c == KC - 1))
                    return ins
                sc.op("pe", mg, [wg_, hT], [pg])
                sc.op("pe", mu, [wu_, hT], [pu])
                sg = tmpf()
                ab = abuf[abstate["i"] % 2]
                abstate["i"] += 1
                sc.op("act", lambda h, pg=pg, sg=sg: h.activation(out=sg.t[:], in_=pg.t[:, 0:512], func=AF.Silu), [pg], [sg])
                sc.op("dve", lambda h, pu=pu, sg=sg, ab=ab: h.tensor_tensor(out=ab.t[:], in0=sg.t[:], in1=pu.t[:, 0:512], op=ALU.mult), [sg, pu], [ab])
                j0 = b0 // P
                transpose_to(lambda n, j0=j0, ch=ch: av.t[:, j0:j0 + 4, ch * P:(ch + 1) * P], [ab.t[:, i * P:(i + 1) * P] for i in range(4)],
                             big1, [ab], evac_eng="act")
        as_proj("wd%d" % l, av, FC, resid_add_tm)

    def odd_prep(l):
        o = l // 2
        d = od
        sc.dma("sp", d["cw"].t[:], ssd_cw[o], writes=[d["cw"]])
        sc.dma("sp", d["cb"].t[:], ssd_cb[o], writes=[d["cb"]])
        sc.dma("sp", d["dtb"].t[:], ssd_dtb[o], writes=[d["dtb"]])
        sc.dma("sp", d["A"].t[:], ssd_alog[o], writes=[d["A"]])
        sc.dma("sp", d["dsk"].t[:], ssd_d[o].partition_broadcast(P), writes=[d["dsk"]])
        sc.dma("sp", d["nwT"].t[:], ssd_nwT[o], writes=[d["nwT"]])
        sc.op("act", lambda h: h.activation(out=d["A"].t[:], in_=d["A"].t[:], func=AF.Exp), [d["A"]], [d["A"]])
        sc.op("dve", lambda h: h.tensor_scalar(out=d["A"].t[:], in0=d["A"].t[:], scalar1=-1.0, scalar2=None, op0=ALU.mult), [d["A"]], [d["A"]])
        sc.op("dve", lambda h: h.memset(d["A"].t[0:64, :], 1.0), [], [d["A"]])
        sc.op("dve", lambda h: h.memset(d["hist"].t[:], 0.0), [], [d["hist"]])
        sc.op("dve", lambda h: h.memset(d["st"].t[:], 0.0), [], [d["st"]])
        sc.op("dve", lambda h: h.memset(d["stb"].t[:], 0.0), [], [d["stb"]])
        src, _ = wbf["wdt%d" % l]
        sv = src.t.rearrange("(c p) n -> p c n", p=P)
        sc.dma("sp", d["wdt"].t[:, :, 0:64], sv, reads=[src], writes=[d["wdt"]])
        sc.dma("sp", d["wdt"].t[:, :, 64:128], sv, reads=[src], writes=[d["wdt"]])

    def odd_layer_tile(l):
        d = od
        key = "min%d" % l
        xbc = ViewBuf(big1, big1.t[:, 0:48 * T].rearrange("p (c t) -> p c t", t=T))
        yT = ViewBuf(big2, big2.t[:, 0:32 * T].rearrange("p (c t) -> p c t", t=T))

        def z_consume(b0, nb, ch, ps):
            sc.op("act", lambda h: h.activation(out=d["sz"].t[:, ch, b0:b0 + nb], in_=ps.t[:, 0:nb], func=AF.Silu), [ps], [d["sz"]])
        tm_proj(key, 0, 4096, hT, z_consume)

        def xbc_consume(c, ps):
            xr, xc = d["xr"], d["xc"]
            sc.op("act", lambda h: h.activation(out=xr.t[:, 3:3 + T], in_=ps.t[:, 0:T], func=AF.Copy), [ps], [xr])
            sc.op("dve", lambda h: h.tensor_copy(out=xr.t[:, 0:3], in_=d["hist"].t[:, c, :]), [d["hist"]], [xr])
            sc.op("dve", lambda h: h.tensor_copy(out=d["hist"].t[:, c, :], in_=xr.t[:, T:T + 3]), [xr], [d["hist"]])
            sc.op("act", lambda h: h.activation(out=xc.t[:], in_=xr.t[:, 3:3 + T], func=AF.Identity,
                                                scale=d["cw"].t[:, c, 3:4], bias=d["cb"].t[:, c:c + 1]), [xr, d["cw"], d["cb"]], [xc])

            for k in range(3):
                sc.op("dve", lambda h, k=k: h.scalar_tensor_tensor(out=xc.t[:], in0=xr.t[:, k:k + T], scalar=d["cw"].t[:, c, k:k + 1],
                                                                  in1=xc.t[:], op0=ALU.mult, op1=ALU.add), [xr, d["cw"], xc], [xc])
            sc.op("act", lambda h: h.activation(out=xbc.t[:, c, :], in_=xc.t[:], func=AF.Silu), [xc], [big1])
        fm_proj(key, 4096, 48, hT, KC, T, xbc_consume)

        psdt = psum()

        def dtf(h):
            ins = None
            for c in range(KC):
                ins = h.matmul(psdt.t[:, 0:T], lhsT=d["wdt"].t[:, c, :], rhs=hT.t[:, c, :], start=(c == 0), stop=(c == KC - 1))
            return ins
        sc.op("pe", dtf, [d["wdt"], hT], [psdt])
        sc.op("act", lambda h: h.activation(out=d["dtT"].t[:], in_=psdt.t[:, 0:T], func=AF.Exp, bias=d["dtb"].t[:, 0:1]), [psdt, d["dtb"]], [d["dtT"]])
        sc.op("act", lambda h: h.activation(out=d["dtT"].t[:], in_=d["dtT"].t[:], func=AF.Ln, bias=1.0), [d["dtT"]], [d["dtT"]])
        sc.op("dve", lambda h: h.tensor_scalar(out=d["dtT"].t[:], in0=d["dtT"].t[:], scalar1=d["A"].t[:, 0:1], scalar2=None, op0=ALU.mult),
              [d["dtT"], d["A"]], [d["dtT"]])

        LVL = int(DBG[1:]) if DBG.startswith("o") else 99
        for ch in range(NCH if LVL >= 2 else 0):
            c0 = ch * P
            pt = psum()
            sc.op("pe", lambda h, pt=pt: h.transpose(out=pt.t[:, 0:P], in_=d["dtT"].t[:, c0:c0 + P], identity=ident_f), [d["dtT"], consts], [pt])
            sc.op("act", lambda h, pt=pt: h.activation(out=d["dtk"].t[:], in_=pt.t[:, 0:P], func=AF.Copy), [pt], [d["dtk"]])
            pc = psum()
            sc.op("pe", lambda h, pc=pc: h.matmul(pc.t[:, 0:64], lhsT=caus_f, rhs=d["dtk"].t[:, 64:128], start=True, stop=True), [consts, d["dtk"]], [pc])
            sc.op("act", lambda h, pc=pc: h.activation(out=d["cumk"].t[:], in_=pc.t[:, 0:64], func=AF.Copy), [pc], [d["cumk"]])
            sc.op("act", lambda h, pc=pc: h.activation(out=d["Ek"].t[:], in_=pc.t[:, 0:64], func=AF.Exp), [pc], [d["Ek"]])
            pe_ = psum()
            sc.op("pe", lambda h, pe_=pe_: h.matmul(pe_.t[:, 0:64], lhsT=esel_f, rhs=d["cumk"].t[:], start=True, stop=True), [consts, d["cumk"]], [pe_])
            sc.op("act", lambda h, pe_=pe_: h.activation(out=d["ecC"].t[:], in_=pe_.t[:, 0:64], func=AF.Exp), [pe_], [d["ecC"]])
            sc.op("dve", lambda h, pe_=pe_: h.tensor_tensor(out=d["wend"].t[:], in0=pe_.t[:, 0:64], in1=d["cumk"].t[:], op=ALU.subtract),
                  [pe_, d["cumk"]], [d["wend"]])
            sc.op("act", lambda h: h.activation(out=d["wend"].t[:], in_=d["wend"].t[:], func=AF.Exp), [d["wend"]], [d["wend"]])
            transpose_to(lambda n: d["Bt"].t[:], [xbc.t[:, 32 + g, c0:c0 + P] for g in range(8)], d["Bt"], [big1])
            for g in range(8 if LVL >= 3 else 0):
                sc.op("pool", lambda h, g=g: h.tensor_tensor(out=d["R"].t[:], in0=bcast(d["dtk"].t[:, 64 + 8 * g:64 + 8 * g + 8], 2, P),
                                                           in1=bcast(caus_f, 1, 8), op=ALU.mult), [d["dtk"], consts], [d["R"]])
                for q in range(2):
                    pd = psum()
                    def df(h, q=q, pd=pd):
                        Rf = d["R"].t[:].rearrange("p a t -> p (a t)")
                        h.matmul(pd.t[:, 0:256], lhsT=lst_f, rhs=Rf[:, 512 * q:512 * q + 256], start=True, stop=True)
                        return h.matmul(pd.t[:, 256:512], lhsT=lst_f, rhs=Rf[:, 512 * q + 256:512 * q + 512], start=True, stop=True)
                    sc.op("pe", df, [consts, d["R"]], [pd])
                    sc.op("act", lambda h, q=q, pd=pd: h.activation(out=d["dec"].t[:, 4 * q:4 * q + 4, :], in_=pd.t[:].rearrange("p (a t) -> p a t", t=P),
                                                                  func=AF.Exp), [pd], [d["dec"]])
                if LVL < 4:
                    continue
                pcb = psum()
                sc.op("pe", lambda h, g=g, pcb=pcb: h.matmul(pcb.t[:, 0:P], lhsT=xbc.t[:, 32 + g, c0:c0 + P], rhs=xbc.t[:, 40 + g, c0:c0 + P],
                                                           start=True, stop=True), [big1], [pcb])
                sc.op("dve", lambda h, pcb=pcb: h.tensor_tensor(out=d["CBm"].t[:], in0=pcb.t[:, 0:P], in1=caus_f, op=ALU.mult), [pcb, consts], [d["CBm"]])
                sc.op("dve", lambda h: h.tensor_tensor(out=d["dec"].t[:], in0=d["dec"].t[:], in1=bcast(d["CBm"].t[:], 1, 8), op=ALU.mult),
                      [d["dec"], d["CBm"]], [d["dec"]])
                transpose_to(lambda n: d["xst"].t[:], [xbc.t[:, 4 * g + i, c0:c0 + P] for i in range(4)], d["xst"], [big1])
                xsv = d["xst"].t[:].rearrange("p (h q) -> p h q", q=64)
                Xtv = d["Xt"].t[:].rearrange("p (h q) -> p h q", q=64)
                Xev = d["Xe"].t[:].rearrange("p (h q) -> p h q", q=64)
                Yv = d["Y"].t[:].rearrange("p (h q) -> p h q", q=64)
                sc.op("dve", lambda h, g=g: h.tensor_tensor(out=Xtv, in0=xsv, in1=bcast(d["dtk"].t[:, 8 * g:8 * g + 8], 2, 64), op=ALU.mult),
                      [d["xst"], d["dtk"]], [d["Xt"]])
                sc.op("pool", lambda h, g=g: h.tensor_tensor(out=Xev, in0=Xtv, in1=bcast(d["wend"].t[:, 8 * g:8 * g + 8], 2, 64), op=ALU.mult),
                      [d["Xt"], d["wend"]], [d["Xe"]])
                if LVL < 5:
                    continue
                pi_, pn_ = psum(), psum()

                def yf(h, pi_=pi_):
                    ins = None
                    for r in range(8):
                        ins = h.matmul(pi_.t[:, 64 * r:64 * r + 64], lhsT=d["dec"].t[:, r, :], rhs=d["Xt"].t[:, 64 * r:64 * r + 64], start=True, stop=True)
                    return ins
                sc.op("pe", yf, [d["dec"], d["Xt"]], [pi_])
                sc.op("pe", lambda h, g=g, pn_=pn_: h.matmul(pn_.t[:, 0:512], lhsT=xbc.t[:, 40 + g, c0:c0 + P], rhs=d["stb"].t[:, 512 * g:512 * g + 512],
                                                           start=True, stop=True), [big1, d["stb"]], [pn_])
                sc.op("dve", lambda h, g=g, pn_=pn_: h.tensor_tensor(out=Yv, in0=pn_.t[:].rearrange("p (h q) -> p h q", q=64),
                                                                   in1=bcast(d["Ek"].t[:, 8 * g:8 * g + 8], 2, 64), op=ALU.mult), [pn_, d["Ek"]], [d["Y"]])
                sc.op("dve", lambda h, pi_=pi_: h.tensor_tensor(out=d["Y"].t[:], in0=d["Y"].t[:], in1=pi_.t[:, 0:512], op=ALU.add), [pi_, d["Y"]], [d["Y"]])
                t_ = tmpf()
                sc.op("pool", lambda h, g=g, t_=t_: h.tensor_tensor(out=t_.t[:].rearrange("p (h q) -> p h q", q=64), in0=xsv,
                                                                  in1=bcast(d["dsk"].t[:, 8 * g:8 * g + 8], 2, 64), op=ALU.mult), [d["xst"], d["dsk"]], [t_])
                sc.op("dve", lambda h, t_=t_: h.tensor_tensor(out=d["Y"].t[:], in0=d["Y"].t[:], in1=t_.t[:], op=ALU.add), [t_, d["Y"]], [d["Y"]])
                sc.op("dve", lambda h, g=g: h.tensor_tensor(out=d["Y"].t[:], in0=d["Y"].t[:], in1=d["sz"].t[:, ch, 512 * g:512 * g + 512], op=ALU.mult),
                      [d["Y"], d["sz"]], [d["Y"]])
                t2_ = tmpf()
                sc.op("act", lambda h, t2_=t2_: h.activation(out=t2_.t[:], in_=d["Y"].t[:], func=AF.Square, accum_out=d["ss"].t[:, 0:1]), [d["Y"]], [t2_, d["ss"]])
                sc.op("act", lambda h: h.activation(out=d["ss"].t[:], in_=d["ss"].t[:], func=AF.Sqrt, scale=1.0 / 512, bias=1e-6), [d["ss"]], [d["ss"]])
                sc.op("dve", lambda h: h.reciprocal(out=d["ss"].t[:], in_=d["ss"].t[:]), [d["ss"]], [d["ss"]])
                sc.op("dve", lambda h: h.tensor_scalar(out=d["Yn"].t[:], in0=d["Y"].t[:], scalar1=d["ss"].t[:, 0:1], scalar2=None, op0=ALU.mult),
                      [d["Y"], d["ss"]], [d["Yn"]])
                if LVL < 6:
                    continue
                ps = psum()
                pv = ps.t[:].bitcast(BF16)

                def tf(h, pv=pv):
                    ins = None
                    for i in range(4):
                        ins = h.transpose(out=pv[:, i * P:(i + 1) * P], in_=d["Yn"].t[:, i * P:(i + 1) * P], identity=ident_b)
                    return ins
                sc.op("pe", tf, [d["Yn"], cbf], [ps])

                def ef(h, g=g, pv=pv):
                    ins = None
                    for i in range(4):
                        ins = h.tensor_scalar(out=yT.t[:, 4 * g + i, c0:c0 + P], in0=pv[:, i * P:(i + 1) * P], scalar1=d["nwT"].t[:, 4 * g + i:4 * g + i + 1],
                                              scalar2=None, op0=ALU.mult)
                    return ins
                sc.op("dve", ef, [ps, d["nwT"]], [big2])
                if LVL < 7:
                    continue
                stv = d["st"].t[:, 512 * g:512 * g + 512].rearrange("p (h q) -> p h q", q=64)
                sc.op("pool", lambda h, g=g, stv=stv: h.tensor_tensor(out=stv, in0=stv, in1=bcast(d["ecC"].t[:, 8 * g:8 * g + 8], 2, 64), op=ALU.mult),
                      [d["st"], d["ecC"]], [d["st"]])
                pS = psum()
                sc.op("pe", lambda h, g=g, pS=pS: h.matmul(pS.t[:, 0:512], lhsT=d["Bt"].t[:, g, :], rhs=d["Xe"].t[:], start=True, stop=True),
                      [d["Bt"], d["Xe"]], [pS])
                sc.op("dve", lambda h, g=g, pS=pS: h.tensor_tensor(out=d["st"].t[:, 512 * g:512 * g + 512], in0=d["st"].t[:, 512 * g:512 * g + 512],
                                                                 in1=pS.t[:, 0:512], op=ALU.add), [pS, d["st"]], [d["st"]])
                sc.op("act", lambda h, g=g: h.activation(out=d["stb"].t[:, 512 * g:512 * g + 512], in_=d["st"].t[:, 512 * g:512 * g + 512], func=AF.Copy),
                      [d["st"]], [d["stb"]])
        as_proj("mout%d" % l, yT, 32, resid_add_tm)

    first = True
    for l in layers:
        sc.barrier()
        if l % 2 == 0:
            even_prep(l)
        else:
            odd_prep(l)
        xattn_prep(l)
        for ti in range(NT):
            t0 = ti * T
            src = xT_in if first else xT_d.t
            sc.dma("sp", xT.t[:], src.rearrange("c p t -> p c t")[:, :, t0:t0 + T], reads=([] if first else [xT_d]), writes=[xT])
            if "m" in STAGES:
                rmsnorm_T(xT, l, T, hT)
                if l % 2 == 0:
                    even_layer_tile(l, ti == 0)
                else:
                    odd_layer_tile(l)
            if "x" in STAGES:
                xattn_tile(l)
            if "f" in STAGES:
                ffn_tile(l)
            sc.dma("sp", xT_d.t.rearrange("c p t -> p c t")[:, :, t0:t0 + T], xT.t[:], reads=[xT], writes=[xT_d])
        first = False
    sc.barrier()
    astate["off"] = 0
    fo = ar("fo", [P, KC, T])
    for ti in range(NT):
        t0 = ti * T
        src = xT_in if first else xT_d.t
        sc.dma("sp", xT.t[:], src.rearrange("c p t -> p c t")[:, :, t0:t0 + T], reads=([] if first else [xT_d]), writes=[xT])
        sc.op("act", lambda h: h.activation(out=sq.t[:, :, 0:T], in_=xT.t[:], func=AF.Square), [xT], [sq])
        ps = psum()

        def mmf(h, ps=ps):
            ins = None
            for c in range(KC):
                ins = h.matmul(ps.t[:, 0:T], lhsT=ones_b, rhs=sq.t[:, c, 0:T], start=(c == 0), stop=(c == KC - 1))
            return ins
        sc.op("pe", mmf, [sq, cbf], [ps])
        sc.op("act", lambda h, ps=ps: h.activation(out=rstd.t[:, 0:T], in_=ps.t[:, 0:T], func=AF.Sqrt, scale=1.0 / D, bias=1e-6), [ps], [rstd])
        sc.op("dve", lambda h: h.reciprocal(out=rstd.t[:, 0:T], in_=rstd.t[:, 0:T]), [rstd], [rstd])

        def nf(h):
            ins = None
            for c in range(KC):
                ins = h.scalar_tensor_tensor(out=fo.t[:, c, :], in0=xT.t[:, c, :], scalar=gv.t[:, 13, c:c + 1],
                                             in1=rstd.t[:, 0:T], op0=ALU.mult, op1=ALU.mult)
            return ins
        sc.op("dve", nf, [xT, rstd, gv], [fo])
        sc.dma("sp", out_T.rearrange("c p t -> p c t")[:, :, t0:t0 + T], fo.t[:], reads=[fo])
    sc.finish()
    block = es.enter_context(nc.Block())
    sc.emit(block)
    es.close()
    return nc, sc


class ViewBuf:
    def __init__(self, parent, ap):
        self._p = parent
        self.t = ap

    @property
    def w(self):
        return self._p.w

    @w.setter
    def w(self, v):
        self._p.w = v

    @property
    def r(self):
        return self._p.r

    @r.setter
    def r(self, v):
        self._p.r = v


def to_fm(a):
    n = a.shape[0]
    return np.ascontiguousarray(a.T.reshape(KC, P, n))


def vec_fm(v, nchunk):
    return np.ascontiguousarray(v.reshape(nchunk, P).T)


def prep_shared(inp, T):
    f = np.float32
    sh = {}
    sh["consts_in"] = make_consts(T)
    sh["g_mix"] = np.stack([vec_fm(inp["norm_mix"][l], KC) for l in range(4)]).astype(f)
    sh["g_xat"] = np.stack([vec_fm(inp["norm_xattn"][l], KC) for l in range(4)]).astype(f)
    sh["g_ffn"] = np.stack([vec_fm(inp["norm_ffn"][l], KC) for l in range(4)]).astype(f)
    sh["g_mem"] = vec_fm(inp["norm_mem"], KC)
    sh["g_fin"] = vec_fm(inp["norm_final"], KC)
    sh["ab_w_in"] = inp["ab_w_in"]
    sh["ab_w_out"] = inp["ab_w_out"]
    sh["lru_cw"] = np.ascontiguousarray(np.stack([inp["lru_conv_w"][e].reshape(4, 8, P).transpose(2, 1, 0) for e in range(2)]))
    for k_, src in (("lru_cb", "lru_conv_b"), ("lru_br", "lru_b_r"), ("lru_bi", "lru_b_i"), ("lru_lam", "lru_lambda"), ("hg_lb", "hgrn_lower_bounds")):
        sh[k_] = np.stack([vec_fm(inp[src][e], 8) for e in range(2)])
    sh["lru_wr"] = inp["lru_w_r"]
    sh["lru_wi"] = inp["lru_w_i"]
    sh["hg_norm"] = inp["hgrn_norm"]
    sh["ssd_w_in"] = inp["ssd_w_in"]
    sh["ssd_w_out"] = inp["ssd_w_out"]
    sh["ssd_cw"] = np.ascontiguousarray(np.stack([inp["ssd_conv_w"][o].reshape(4, 48, P).transpose(2, 1, 0) for o in range(2)]))
    sh["ssd_cb"] = np.stack([vec_fm(inp["ssd_conv_b"][o], 48) for o in range(2)])
    sh["ssd_dtb"] = np.stack([np.concatenate([inp["ssd_dt_bias"][o], inp["ssd_dt_bias"][o]]).reshape(P, 1) for o in range(2)])
    sh["ssd_alog"] = np.stack([np.concatenate([inp["ssd_a_log"][o], inp["ssd_a_log"][o]]).reshape(P, 1) for o in range(2)])
    sh["ssd_d"] = inp["ssd_d"]
    sh["ssd_nwT"] = np.stack([vec_fm(inp["ssd_norm"][o], 32) for o in range(2)])
    sh["xa_wq"] = inp["xa_w_q"]
    sh["xa_wkv"] = inp["xa_w_kv"]
    sh["xa_wo"] = inp["xa_w_o"]
    sh["ffn_wg"] = inp["ffn_w_gate"]
    sh["ffn_wu"] = inp["ffn_w_up"]
    sh["ffn_wd"] = inp["ffn_w_down"]
    return {k_: np.ascontiguousarray(np.asarray(v, dtype=f)) for k_, v in sh.items()}


_T = 128
STAGES = "mxf"


def kernel(**inputs):
    inp = {k: np.asarray(v) for k, v in inputs.items()}
    x = inp["x"]
    mem = inp["mem"]
    B, S, _ = x.shape
    sh = prep_shared(inp, _T)
    nc, sc = build(S, [0, 1, 2, 3], T=_T)
    in_maps = []
    for b in range(B):
        m = dict(sh)
        m["xT_in"] = to_fm(x[b])
        m["memT_in"] = to_fm(mem[b])
        in_maps.append(m)
    res = run_bass_kernel_spmd(nc, in_maps, core_ids=list(range(B)))
    out = np.empty((B, S, D), np.float32)
    for b in range(B):
        o = res.results[b]["out_T"]
        out[b] = o.reshape(D, S).T
    return out
```
